# Optimizing a Trainium2 kernel written in Bass

```python
import math
import jax, jax.numpy as jnp
from jax import lax
import numpy as np

D_MODEL = 2048
BATCH = 2
SEQ = 8192
DEPTH = 4

N_MIXERS = 4
BLOCK = 128
EPS = 1e-6
ROPE_THETA = 10000.0
CONV_WIDTH = 3
SGU_WIDTH = D_MODEL
SGU_GROUPS = 16
SGU_CHUNK = 128
DIFF_HEADS = 16
DIFF_HEAD_DIM = D_MODEL // DIFF_HEADS // 2
SB_HEADS = 16
SB_HEAD_DIM = D_MODEL // SB_HEADS
D_FF = 5632
FFN_CONV_WIDTH = 3
N_A = (DEPTH + 3) // 4
N_B = (DEPTH + 2) // 4
N_C = (DEPTH + 1) // 4
N_D = DEPTH // 4

kernel_name = 'interleaved_hybrid_conv_sgu_diffattn_stickbreak'


def rms_norm(x, g):
    xf = x.astype(jnp.float32)
    y = xf * lax.rsqrt(jnp.mean(xf * xf, axis=-1, keepdims=True) + EPS)
    return (y * g.astype(jnp.float32)).astype(x.dtype)


def layer_norm(x, g, b):
    xf = x.astype(jnp.float32)
    mu = jnp.mean(xf, axis=-1, keepdims=True)
    xc = xf - mu
    y = xc * lax.rsqrt(jnp.mean(xc * xc, axis=-1, keepdims=True) + EPS)
    return (y * g.astype(jnp.float32) + b.astype(jnp.float32)).astype(x.dtype)


def causal_dwconv(x, w):
    width, ch = w.shape
    return lax.conv_general_dilated(
        x, w[:, None, :].astype(x.dtype), window_strides=(1,), padding=[(width - 1, 0)],
        dimension_numbers=('NWC', 'WIO', 'NWC'), feature_group_count=ch)


def rope_tables(positions, dim):
    inv_freq = 1.0 / (ROPE_THETA ** (jnp.arange(0, dim, 2, dtype=jnp.float32) / dim))
    ang = positions.astype(jnp.float32)[..., None] * inv_freq
    return jnp.cos(ang)[:, :, None, :], jnp.sin(ang)[:, :, None, :]


def apply_rope(x, cos, sin):
    x1, x2 = jnp.split(x.astype(jnp.float32), 2, axis=-1)
    return jnp.concatenate([x1 * cos - x2 * sin, x2 * cos + x1 * sin], axis=-1).astype(x.dtype)


def to_query_blocks(t):
    b, s = t.shape[:2]
    return jnp.moveaxis(t.reshape(b, s // BLOCK, BLOCK, *t.shape[2:]), 1, 0)


def from_query_blocks(t):
    t = jnp.moveaxis(t, 0, 1)
    return t.reshape(t.shape[0], t.shape[1] * t.shape[2], *t.shape[3:])


def short_conv_mixer(h, w_in, conv_w, w_out):
    gate_b, gate_c, xin = jnp.split(h @ w_in, 3, axis=-1)
    return (gate_b * causal_dwconv(gate_c * xin, conv_w)) @ w_out


def spatial_gating_mixer(h, w_in, ln_g, ln_b, w_s, b_s, w_out):
    b, s, _ = h.shape
    u, v = jnp.split(jax.nn.gelu(h @ w_in), 2, axis=-1)
    v = layer_norm(v, ln_g, ln_b)
    v = v.reshape(b, s // SGU_CHUNK, SGU_CHUNK, SGU_GROUPS, SGU_WIDTH // SGU_GROUPS)
    causal = jnp.tril(jnp.ones((SGU_CHUNK, SGU_CHUNK), dtype=bool))
    w_causal = jnp.where(causal[None], w_s, 0.0)
    mixed = jnp.einsum('gts,bcsgd->bctgd', w_causal, v) + b_s.T[:, :, None]
    return (u * mixed.reshape(b, s, SGU_WIDTH)) @ w_out


def differential_attention(h, w_qkv, lq1, lk1, lq2, lk2, subln_g, w_out, cos, sin, lambda_init):
    b, s, _ = h.shape
    H, d = DIFF_HEADS, DIFF_HEAD_DIM
    q, k, v = jnp.split(h @ w_qkv, 3, axis=-1)
    q = apply_rope(q.reshape(b, s, 2 * H, d), cos, sin).reshape(b, s, H, 2, d)
    k = apply_rope(k.reshape(b, s, 2 * H, d), cos, sin).reshape(b, s, H, 2, d)
    v = v.reshape(b, s, H, 2 * d)
    lam = (jnp.exp(jnp.sum(lq1.astype(jnp.float32) * lk1.astype(jnp.float32)))
           - jnp.exp(jnp.sum(lq2.astype(jnp.float32) * lk2.astype(jnp.float32))) + lambda_init)
    scale = d ** -0.5
    k_pos = jnp.arange(s)

    def block(args):
        qb, i = args
        scores = jnp.einsum('bqhcd,bkhcd->bhcqk', qb, k).astype(jnp.float32) * scale
        q_pos = i * BLOCK + jnp.arange(BLOCK)
        mask = k_pos[None, :] <= q_pos[:, None]
        p = jax.nn.softmax(jnp.where(mask, scores, -jnp.inf), axis=-1)
        attn = p[:, :, 0] - lam * p[:, :, 1]
        return jnp.einsum('bhqk,bkhe->bqhe', attn.astype(v.dtype), v)

    o = from_query_blocks(lax.map(block, (to_query_blocks(q), jnp.arange(s // BLOCK))))
    o = rms_norm(o, subln_g) * (1.0 - lambda_init)
    return o.reshape(b, s, H * 2 * d) @ w_out


def stick_breaking_attention(h, w_qkv, w_out):
    b, s, _ = h.shape
    H, d = SB_HEADS, SB_HEAD_DIM
    q, k, v = [t.reshape(b, s, H, d) for t in jnp.split(h @ w_qkv, 3, axis=-1)]
    scale = d ** -0.5
    k_pos = jnp.arange(s)

    def block(args):
        qb, i = args
        z = jnp.einsum('bqhd,bkhd->bhqk', qb, k).astype(jnp.float32) * scale
        q_pos = i * BLOCK + jnp.arange(BLOCK)
        mask = k_pos[None, :] < q_pos[:, None]
        log_beta = jax.nn.log_sigmoid(z)
        log_1m = jnp.where(mask, log_beta - z, 0.0)
        log_survive = lax.cumsum(log_1m, axis=log_1m.ndim - 1, reverse=True) - log_1m
        a = jnp.where(mask, jnp.exp(log_beta + log_survive), 0.0)
        return jnp.einsum('bhqk,bkhd->bqhd', a.astype(v.dtype), v)

    o = from_query_blocks(lax.map(block, (to_query_blocks(q), jnp.arange(s // BLOCK))))
    return o.reshape(b, s, H * d) @ w_out


def conv_ffn(h, w_gate, w_up, conv_w, conv_b, w_down):
    g = causal_dwconv(h @ w_gate, conv_w) + conv_b
    return (jax.nn.silu(g) * (h @ w_up)) @ w_down


def setup_inputs(seed: int = 0) -> dict:
    key = jax.random.key(seed)
    ks = iter(jax.random.split(key, 32))
    f32 = jnp.float32

    def w(shape, fan_in):
        return jax.random.normal(next(ks), shape, f32) * fan_in ** -0.5

    def gain(shape):
        return 1.0 + 0.02 * jax.random.normal(next(ks), shape, f32)

    def small(shape, s=0.02):
        return s * jax.random.normal(next(ks), shape, f32)

    D = D_MODEL
    dH, dd = DIFF_HEADS, DIFF_HEAD_DIM
    return {
        'x': jax.random.normal(next(ks), (BATCH, SEQ, D), f32),
        'positions': jnp.tile(jnp.arange(SEQ, dtype=jnp.int32)[None, :], (BATCH, 1)),
        'norm_mix_g': gain((DEPTH, D)),
        'norm_ffn_g': gain((DEPTH, D)),
        'norm_final_g': gain((D,)),
        'sc_w_in': w((N_A, D, 3 * D), D),
        'sc_conv_w': w((N_A, CONV_WIDTH, D), CONV_WIDTH),
        'sc_w_out': w((N_A, D, D), D),
        'sg_w_in': w((N_B, D, 2 * SGU_WIDTH), D),
        'sg_ln_g': gain((N_B, SGU_WIDTH)),
        'sg_ln_b': small((N_B, SGU_WIDTH)),
        'sg_w_s': w((N_B, SGU_GROUPS, SGU_CHUNK, SGU_CHUNK), SGU_CHUNK),
        'sg_b_s': gain((N_B, SGU_GROUPS, SGU_CHUNK)),
        'sg_w_out': w((N_B, SGU_WIDTH, D), SGU_WIDTH),
        'da_w_qkv': w((N_C, D, 6 * dH * dd), D),
        'da_lambda_q1': small((N_C, dd), 0.1),
        'da_lambda_k1': small((N_C, dd), 0.1),
        'da_lambda_q2': small((N_C, dd), 0.1),
        'da_lambda_k2': small((N_C, dd), 0.1),
        'da_subln_g': gain((N_C, 2 * dd)),
        'da_w_out': w((N_C, 2 * dH * dd, D), 2 * dH * dd),
        'sb_w_qkv': w((N_D, D, 3 * SB_HEADS * SB_HEAD_DIM), D),
        'sb_w_out': w((N_D, SB_HEADS * SB_HEAD_DIM, D), SB_HEADS * SB_HEAD_DIM),
        'ffn_w_gate': w((DEPTH, D, D_FF), D),
        'ffn_w_up': w((DEPTH, D, D_FF), D),
        'ffn_conv_w': w((DEPTH, FFN_CONV_WIDTH, D_FF), FFN_CONV_WIDTH),
        'ffn_conv_b': small((DEPTH, D_FF)),
        'ffn_w_down': w((DEPTH, D_FF, D), D_FF),
    }


def reference(x, positions, norm_mix_g, norm_ffn_g, norm_final_g,
              sc_w_in, sc_conv_w, sc_w_out,
              sg_w_in, sg_ln_g, sg_ln_b, sg_w_s, sg_b_s, sg_w_out,
              da_w_qkv, da_lambda_q1, da_lambda_k1, da_lambda_q2, da_lambda_k2, da_subln_g, da_w_out,
              sb_w_qkv, sb_w_out,
              ffn_w_gate, ffn_w_up, ffn_conv_w, ffn_conv_b, ffn_w_down):
    cos, sin = rope_tables(positions, DIFF_HEAD_DIM)
    h = x
    for layer in range(DEPTH):
        mixer, j = layer % N_MIXERS, layer // N_MIXERS
        a = rms_norm(h, norm_mix_g[layer])
        if mixer == 0:
            y = short_conv_mixer(a, sc_w_in[j], sc_conv_w[j], sc_w_out[j])
        elif mixer == 1:
            y = spatial_gating_mixer(a, sg_w_in[j], sg_ln_g[j], sg_ln_b[j], sg_w_s[j], sg_b_s[j], sg_w_out[j])
        elif mixer == 2:
            lambda_init = 0.8 - 0.6 * math.exp(-0.3 * layer)
            y = differential_attention(a, da_w_qkv[j], da_lambda_q1[j], da_lambda_k1[j],
                                       da_lambda_q2[j], da_lambda_k2[j], da_subln_g[j], da_w_out[j],
                                       cos, sin, lambda_init)
        else:
            y = stick_breaking_attention(a, sb_w_qkv[j], sb_w_out[j])
        h = h + y
        h = h + conv_ffn(rms_norm(h, norm_ffn_g[layer]), ffn_w_gate[layer], ffn_w_up[layer],
                         ffn_conv_w[layer], ffn_conv_b[layer], ffn_w_down[layer])
    return rms_norm(h, norm_final_g)
```

```python
import math
import numpy as np
import ml_dtypes
import concourse.bass as bass
import concourse.mybir as mybir
from concourse.bass_utils import run_bass_kernel_spmd

F32 = mybir.dt.float32
BF16 = mybir.dt.bfloat16
I32 = mybir.dt.int32
AF = mybir.ActivationFunctionType
ALU = mybir.AluOpType
AX = mybir.AxisListType

D = 2048
DC = 16
DFF = 5632
FC = 44
SEQ = 8192
NCORE = 8
OWN = 2048
HALO = 256
NTOK = OWN + HALO
TB = 384
SB = 384
NBLK = NTOK // TB
NSUB = TB // SB
EPS = 1e-6
NKT = SEQ // 128
NEG = -30000.0
NCW = 256
NVG = D // NCW
SEM_LIMIT = 30000


class Ev:
    __slots__ = ("sem", "val", "key")

    def __init__(self, sem, val, key):
        self.sem, self.val, self.key = sem, val, key


class T:
    __slots__ = ("ap", "name", "w", "r", "nowaw")

    def __init__(self, ap, name="", nowaw=False):
        self.ap, self.name, self.w, self.r, self.nowaw = ap, name, {}, {}, nowaw

    def __getitem__(self, k):
        return self.ap[k]


class Chan:
    def __init__(self, kb, name):
        self.kb, self.name = kb, name
        self.sem = kb.new_sem("c_" + name)
        self.cnt = 0

    def next_ev(self):
        self.cnt += 16
        assert self.cnt < 60000, self.name
        return Ev(self.sem, self.cnt, id(self.sem))


class Engine:
    def __init__(self, kb, name, eng, selfwait=True):
        self.kb, self.name, self.eng, self.selfwait = kb, name, eng, selfwait
        self.sem = kb.new_sem("e_" + name)
        self.cnt = 0
        self.seen = {}
        self.nins = 0
        self.trace = []

    def wait(self, ev):
        if ev.key == id(self.sem) and not self.selfwait:
            return
        if self.seen.get(ev.key, 0) >= ev.val:
            return
        self.eng.wait_ge(ev.sem, ev.val)
        self.seen[ev.key] = ev.val
        self.trace.append(("w", ev.key, ev.val))

    def deps(self, reads, writes):
        for t in reads:
            for ev in t.w.values():
                self.wait(ev)
        for t in writes:
            if t.nowaw:
                continue
            for ev in t.w.values():
                self.wait(ev)
            for ev in t.r.values():
                self.wait(ev)

    @staticmethod
    def mark(ev, reads, writes):
        for t in reads:
            o = t.r.get(ev.key)
            if o is None or o.val < ev.val:
                t.r[ev.key] = ev
        for t in writes:
            if t.nowaw:
                t.w[ev.key] = ev
            else:
                t.w = {ev.key: ev}
                t.r = {}

    def op(self, fn, reads=(), writes=()):
        self.deps(reads, writes)
        ins = fn(self.eng)
        if self.cnt >= SEM_LIMIT:
            self.sem = self.kb.new_sem("e_" + self.name)
            self.cnt = 0
        self.cnt += 1
        ins.then_inc(self.sem, 1)
        self.trace.append(("i", id(self.sem), 1))
        ev = Ev(self.sem, self.cnt, id(self.sem))
        self.mark(ev, reads, writes)
        self.nins += 1
        return ev

    def dma(self, chan, out_ap, in_ap, reads=(), writes=()):
        self.deps(reads, writes)
        ev = chan.next_ev()
        self.eng.dma_start(out=out_ap, in_=in_ap).then_inc(chan.sem, 16)
        self.trace.append(("i", id(chan.sem), 16))
        self.mark(ev, reads, writes)
        self.nins += 1
        return ev


class KB:
    def __init__(self):
        self.nc = bass.Bass("TRN2", target_bir_lowering=False)
        self._ctx = []
        self._perm = []
        self.ag_evs = []
        self.nsem = 0
        nc = self.nc
        self.pe = Engine(self, "pe", nc.tensor, selfwait=False)
        self.act = Engine(self, "act", nc.scalar)
        self.dve = Engine(self, "dve", nc.vector)
        self.pool = Engine(self, "pool", nc.gpsimd)
        self.sp = Engine(self, "sp", nc.sync)
        self.engines = [self.pe, self.act, self.dve, self.pool, self.sp]
        self.chans = []
        self.ps_tiles = []
        self.ps_pool = list(range(8))
        self.ps_i = 0
        self.final_evs = []

    def enter(self, cm):
        v = cm.__enter__()
        self._ctx.append(cm)
        return v

    def close(self):
        for cm in reversed(self._ctx):
            cm.__exit__(None, None, None)
        self._ctx = []
        for cm in reversed(self._perm):
            cm.__exit__(None, None, None)
        self._perm = []

    def new_sem(self, name):
        self.nsem += 1
        cm = self.nc.semaphore(f"{name}_{self.nsem}")
        v = cm.__enter__()
        self._perm.append(cm)
        return v

    def mark(self):
        return len(self._ctx)

    def release(self, m, include_ag=True):
        self.barrier(include_ag)
        while len(self._ctx) > m:
            self._ctx.pop().__exit__(None, None, None)

    def allgather(self, groups, in_T, out_T):
        sem = self.new_sem("ag")
        self.pool.deps([in_T], [out_T])
        self.pool.eng.collective_compute("AllGather", ALU.bypass, replica_groups=groups,
                                         ins=[in_T.ap], outs=[out_T.ap]).then_inc(sem, 1)
        ev = Ev(sem, 1, id(sem))
        self.pool.trace.append(("i", id(sem), 1))
        Engine.mark(ev, [in_T], [out_T])
        self.ag_evs.append(ev)
        return ev

    def chan(self, name):
        c = Chan(self, name)
        self.chans.append(c)
        return c

    def sb(self, name, shape, dt):
        return self.enter(self.nc.sbuf_tensor(name, list(shape), dt))

    def din(self, name, shape, dt):
        return T(self.nc.dram_tensor(name, list(shape), dt, kind="ExternalInput").ap(), name)

    def dout(self, name, shape, dt):
        return T(self.nc.dram_tensor(name, list(shape), dt, kind="ExternalOutput").ap(), name, nowaw=True)

    def dscr(self, name, shape, dt):
        return T(self.nc.dram_tensor(name, list(shape), dt).ap(), name, nowaw=True)

    def init_psum(self):
        for i in range(8):
            p = self.enter(self.nc.psum_tensor(f"ps{i}", [128, 512], F32))
            self.ps_tiles.append(T(p, f"ps{i}"))

    def psum(self):
        pool = self.ps_pool
        t = self.ps_tiles[pool[self.ps_i % len(pool)]]
        self.ps_i += 1
        return t

    def barrier(self, include_ag=True):
        evs = []
        for e in self.engines:
            if e.cnt > 0:
                evs.append(Ev(e.sem, e.cnt, id(e.sem)))
        for c in self.chans:
            if c.cnt > 0:
                evs.append(Ev(c.sem, c.cnt, id(c.sem)))
        if include_ag:
            evs.extend(self.ag_evs)
        for e in self.engines:
            for ev in evs:
                if ev.key == id(e.sem):
                    continue
                e.wait(ev)

    def simulate(self):
        vals = {}
        pcs = {e.name: 0 for e in self.engines}
        progress = True
        while progress:
            progress = False
            for e in self.engines:
                tr = e.trace
                pc = pcs[e.name]
                while pc < len(tr):
                    k, key, v = tr[pc]
                    if k == "w":
                        if vals.get(key, 0) >= v:
                            pc += 1
                            progress = True
                        else:
                            break
                    else:
                        vals[key] = vals.get(key, 0) + v
                        pc += 1
                        progress = True
                pcs[e.name] = pc
        stuck = {e.name: (pcs[e.name], len(e.trace)) for e in self.engines if pcs[e.name] < len(e.trace)}
        return stuck, max(vals.values())

    def sync_chan(self, c):
        ev = Ev(c.sem, c.cnt, id(c.sem))
        for e in self.engines:
            e.wait(ev)

    def mm(self, ps_ap, pairs, reads, pst):
        n = len(pairs)

        def fn(eng):
            ins = None
            for i, (l, r) in enumerate(pairs):
                ins = eng.matmul(ps_ap, lhsT=l, rhs=r, start=(i == 0), stop=(i == n - 1))
            return ins
        return self.pe.op(fn, reads=reads, writes=[pst])


def col_layout():
    off = {}
    n = 0

    def add(name, k):
        nonlocal n
        off[name] = n
        n += k
    for l in range(4):
        add(f"gm{l}", DC)
        add(f"gf{l}", DC)
        add(f"fcw{l}", 3 * FC)
        add(f"fcb{l}", FC)
    add("gfin", DC)
    add("scw", 3 * DC)
    add("subg", 1)
    add("invf", 1)
    add("sgn", 1)
    add("kpos", NKT)
    add("lam", 8)
    add("lngc", DC)
    add("lnbc", DC)
    return off, n


COLS, NCOLS = col_layout()


def build_cols(inp):
    c = np.zeros((128, NCOLS), np.float32)

    def colsof(v):
        return np.ascontiguousarray(v.reshape(-1, 128).T)
    for l in range(4):
        c[:, COLS[f"gm{l}"]:COLS[f"gm{l}"] + DC] = colsof(inp["norm_mix_g"][l])
        c[:, COLS[f"gf{l}"]:COLS[f"gf{l}"] + DC] = colsof(inp["norm_ffn_g"][l])
        for k in range(3):
            o = COLS[f"fcw{l}"] + k * FC
            c[:, o:o + FC] = colsof(inp["ffn_conv_w"][l, k])
        c[:, COLS[f"fcb{l}"]:COLS[f"fcb{l}"] + FC] = colsof(inp["ffn_conv_b"][l])
    c[:, COLS["gfin"]:COLS["gfin"] + DC] = colsof(inp["norm_final_g"])
    for k in range(3):
        o = COLS["scw"] + k * DC
        c[:, o:o + DC] = colsof(inp["sc_conv_w"][0, k])
    c[:, COLS["subg"]] = inp["da_subln_g"][0]
    c[:, COLS["lngc"]:COLS["lngc"] + DC] = colsof(inp["sg_ln_g"][0])
    c[:, COLS["lnbc"]:COLS["lnbc"] + DC] = colsof(inp["sg_ln_b"][0])
    inv_freq = (1.0 / (10000.0 ** (np.arange(0, 64, 2, dtype=np.float32) / np.float32(64)))).astype(np.float32)
    p = np.arange(128)
    c[:, COLS["invf"]] = (inv_freq[p % 32].astype(np.float64) / (2 * np.pi)).astype(np.float32)
    c[:, COLS["kpos"]:COLS["kpos"] + NKT] = (np.arange(NKT)[None, :] * 128 + p[:, None]).astype(np.float32)
    return c


def build_consts():
    k = np.arange(128)
    m = np.zeros((128, 5, 128), np.float32)
    m[:, 0, :] = 1.0
    m[:, 1, :] = (k[None, :] >= k[:, None])
    R = np.zeros((128, 128), np.float32)
    for mm_ in range(128):
        if mm_ % 64 < 32:
            R[mm_ + 32, mm_] = -1.0
        else:
            R[mm_ - 32, mm_] = 1.0
    m[:, 2, :] = R
    m[:, 3, :] = (k[:, None] >= k[None, :])
    m[:, 4, :] = np.eye(128)
    return m


class Prog:
    def __init__(self, phases):
        self.phases = phases
        self.kb = KB()
        self.build()

    def declare_weight(self, name, K, N, mode):
        kb = self.kb
        C = K // 128
        cw = 128 if mode == "T" else NCW
        nt = -(-(N // cw) // NCORE) * NCORE
        w = kb.din(name, [K, nt * cw // NCORE], F32)
        loc = kb.dscr(name + "_loc", [nt // NCORE * 128, C * cw], BF16)
        full = kb.dscr(name + "_bf", [nt * 128, C * cw], BF16)
        self.wts[name] = (w, loc, full, K, nt * cw, mode)

    def convert_weights(self, names):
        kb = self.kb
        STG = 22 * 256
        m = kb.mark()
        s32 = [T(kb.sb(f"stg32{i}", [128, STG], F32), f"s32{i}") for i in range(2)]
        s16 = [T(kb.sb(f"stg16{i}", [128, STG], BF16), f"s16{i}") for i in range(2)]
        ch_in = [kb.chan("cvin0"), kb.chan("cvin1")]
        ch_out = [kb.chan("cvout0"), kb.chan("cvout1")]
        it = 0
        casters = [kb.pool, kb.dve, kb.act]
        wcols = 256
        for name in names:
            w, loc, full, K, N, mode = self.wts[name]
            C = K // 128
            cw = 128 if mode == "T" else NCW
            nloc = N // NCORE
            for n0 in range(0, nloc, wcols):
                i = it % 2
                src = w.ap[:, n0:n0 + wcols].rearrange("(c p) n -> p c n", p=128)
                dst32 = s32[i].ap[:, 0:C * wcols].rearrange("p (c n) -> p c n", c=C)
                qeng = kb.sp if it % 2 == 0 else kb.pool
                qeng.dma(ch_in[i], dst32, src, reads=[w], writes=[s32[i]])
                ce = casters[it % 3]
                a16 = s16[i].ap[:, 0:C * wcols]
                a32 = s32[i].ap[:, 0:C * wcols]
                if ce is kb.act:
                    ce.op(lambda e, a16=a16, a32=a32: e.activation(out=a16, in_=a32, func=AF.Copy), reads=[s32[i]], writes=[s16[i]])
                else:
                    ce.op(lambda e, a16=a16, a32=a32: e.tensor_copy(out=a16, in_=a32), reads=[s32[i]], writes=[s16[i]])
                nt = wcols // cw
                t0 = n0 // cw
                srcb = s16[i].ap[:, 0:C * wcols].rearrange("p (c t j) -> p t c j", c=C, t=nt)
                qeng2 = kb.pool if it % 2 == 0 else kb.sp
                for t in range(nt):
                    dstb = loc.ap[(t0 + t) * 128:(t0 + t + 1) * 128, :].rearrange("p (c j) -> p c j", c=C)
                    qeng2.dma(ch_out[i], dstb, srcb[:, t], reads=[s16[i]], writes=[loc])
                it += 1
        for name in names:
            w, loc, full, K, N, mode = self.wts[name]
            kb.allgather([list(range(NCORE))], loc, full)
        kb.release(m, include_ag=False)

    def wtile(self, name, t):
        kb = self.kb
        w, loc, full, K, N, mode = self.wts[name]
        C = K // 128
        cw = 128 if mode == "T" else NCW
        nbytes = C * cw * 2
        if nbytes <= 4096:
            ring, idx = self.wsmall, self.wsmall_i
            self.wsmall_i += 1
        else:
            ring, idx = self.wbig, self.wbig_i
            self.wbig_i += 1
        slot, chan = ring[idx % len(ring)]
        kb.sp.dma(chan, slot.ap[:, 0:C * cw], full.ap[t * 128:(t + 1) * 128, :], reads=[full], writes=[slot])
        return slot, slot.ap[:, 0:C * cw].rearrange("p (c j) -> p c j", c=C)

    def col(self, name, i=0):
        o = COLS[name] + i
        return self.cols.ap[:, o:o + 1]

    def rmsnorm(self, gname, inplace=False):
        kb = self.kb
        hT, aT, sq = self.hT, self.aT, self.hid
        if inplace:
            aT = hT
        for s in range(NSUB):
            sl = slice(s * SB, (s + 1) * SB)
            for c in range(DC):
                kb.act.op(lambda e, c=c: e.activation(out=sq[c].ap[:, sl], in_=hT[c].ap[:, sl], func=AF.Square),
                          reads=[hT[c]], writes=[sq[c]])
        for s in range(NSUB):
            sl = slice(s * SB, (s + 1) * SB)
            ps = kb.psum()
            kb.mm(ps.ap[:, 0:SB], [(self.ones_bf, sq[c].ap[:, sl]) for c in range(DC)], reads=[sq[c] for c in range(DC)] + [self.cst], pst=ps)
            r = self.rstd
            kb.dve.op(lambda e: e.tensor_scalar(out=r.ap[:, sl], in0=ps.ap[:, 0:SB], scalar1=1.0 / D, scalar2=EPS,
                                                op0=ALU.mult, op1=ALU.add), reads=[ps], writes=[r])
            kb.dve.op(lambda e: e.reciprocal(out=r.ap[:, sl], in_=r.ap[:, sl]), reads=[r], writes=[r])
            kb.act.op(lambda e: e.activation(out=r.ap[:, sl], in_=r.ap[:, sl], func=AF.Sqrt), reads=[r], writes=[r])
        for s in range(NSUB):
            sl = slice(s * SB, (s + 1) * SB)
            for c in range(DC):
                kb.dve.op(lambda e, c=c: e.scalar_tensor_tensor(out=aT[c].ap[:, sl], in0=hT[c].ap[:, sl], scalar=self.col(gname, c),
                                                                 in1=self.rstd.ap[:, sl], op0=ALU.mult, op1=ALU.mult),
                          reads=[hT[c], self.rstd, self.cols], writes=[aT[c]])

    def proj_resid(self, wname, src, nk):
        kb = self.kb
        for i in range(DC):
            slot, wv = self.wtile(wname, i)
            for s in range(NSUB):
                sl = slice(s * SB, (s + 1) * SB)
                ps = kb.psum()
                kb.mm(ps.ap[:, 0:SB], [(wv[:, k, :], src[k].ap[:, sl]) for k in range(nk)],
                      reads=[slot] + [src[k] for k in range(nk)], pst=ps)
                kb.dve.op(lambda e, i=i, ps=ps, sl=sl: e.tensor_tensor(out=self.hT[i].ap[:, sl], in0=ps.ap[:, 0:SB], in1=self.hT[i].ap[:, sl], op=ALU.add),
                          reads=[ps, self.hT[i]], writes=[self.hT[i]])

    def conv3(self, eng, gs, acc, wname, wi, nch, bias_col=None):
        kb = self.kb
        w0, w1, w2 = self.col(wname, 0 * nch + wi), self.col(wname, 1 * nch + wi), self.col(wname, 2 * nch + wi)
        if bias_col is not None:
            eng.op(lambda e: e.tensor_scalar(out=acc.ap[:, 0:TB], in0=gs.ap[:, 2:TB + 2], scalar1=w2, scalar2=bias_col,
                                             op0=ALU.mult, op1=ALU.add), reads=[gs, self.cols], writes=[acc])
        else:
            eng.op(lambda e: e.tensor_scalar(out=acc.ap[:, 0:TB], in0=gs.ap[:, 2:TB + 2], scalar1=w2, scalar2=None,
                                             op0=ALU.mult), reads=[gs, self.cols], writes=[acc])
        eng.op(lambda e: e.scalar_tensor_tensor(out=acc.ap[:, 0:TB], in0=gs.ap[:, 1:TB + 1], scalar=w1, in1=acc.ap[:, 0:TB],
                                                op0=ALU.mult, op1=ALU.add), reads=[gs, acc, self.cols], writes=[acc])
        eng.op(lambda e: e.scalar_tensor_tensor(out=acc.ap[:, 0:TB], in0=gs.ap[:, 0:TB], scalar=w0, in1=acc.ap[:, 0:TB],
                                                op0=ALU.mult, op1=ALU.add), reads=[gs, acc, self.cols], writes=[acc])

    def ffn(self, l, blk):
        kb = self.kb
        aT, hid = self.aT, self.hid
        self.rmsnorm(f"gf{l}")
        HF = FC // 2
        for half in range(2):
            for jj in range(HF):
                j = half * HF + jj
                gslot, gw = self.wtile(f"wg{l}", j)
                uslot, uw = self.wtile(f"wu{l}", j)
                gs = self.gs[j % 2]
                acc = self.acc[j % 2]
                carry = self.fcarrys[l]
                kb.act.op(lambda e, gs=gs, j=j: e.copy(out=gs.ap[:, 0:2], in_=carry.ap[:, j, :]), reads=[carry], writes=[gs])
                pss = []
                for s in range(NSUB):
                    sl = slice(s * SB, (s + 1) * SB)
                    psg = kb.psum()
                    kb.mm(psg.ap[:, 0:SB], [(gw[:, c, :], aT[c].ap[:, sl]) for c in range(DC)], reads=[gslot] + aT, pst=psg)
                    kb.act.op(lambda e, gs=gs, psg=psg, s=s: e.activation(out=gs.ap[:, 2 + s * SB:2 + (s + 1) * SB], in_=psg.ap[:, 0:SB], func=AF.Copy),
                              reads=[psg], writes=[gs])
                for s in range(NSUB):
                    sl = slice(s * SB, (s + 1) * SB)
                    psu = kb.psum()
                    kb.mm(psu.ap[:, 0:SB], [(uw[:, c, :], aT[c].ap[:, sl]) for c in range(DC)], reads=[uslot] + aT, pst=psu)
                    pss.append(psu)
                kb.act.op(lambda e, gs=gs, j=j: e.copy(out=carry.ap[:, j, :], in_=gs.ap[:, TB:TB + 2]), reads=[gs], writes=[carry])
                self.conv3(kb.dve, gs, acc, f"fcw{l}", j, FC, bias_col=self.col(f"fcb{l}", j))
                kb.act.op(lambda e, acc=acc: e.activation(out=acc.ap[:, 0:TB], in_=acc.ap[:, 0:TB], func=AF.Silu), reads=[acc], writes=[acc])
                for s in range(NSUB):
                    sl = slice(s * SB, (s + 1) * SB)
                    kb.dve.op(lambda e, acc=acc, jj=jj, s=s, sl=sl: e.tensor_tensor(out=hid[jj].ap[:, sl], in0=pss[s].ap[:, 0:SB], in1=acc.ap[:, sl], op=ALU.mult),
                              reads=[pss[s], acc], writes=[hid[jj]])
            for i in range(DC):
                slot, wv = self.wtile(f"wd{l}h{half}", i)
                for s in range(NSUB):
                    sl = slice(s * SB, (s + 1) * SB)
                    ps = kb.psum()
                    kb.mm(ps.ap[:, 0:SB], [(wv[:, k, :], hid[k].ap[:, sl]) for k in range(HF)], reads=[slot] + hid[:HF], pst=ps)
                    kb.dve.op(lambda e, i=i, ps=ps, sl=sl: e.tensor_tensor(out=self.hT[i].ap[:, sl], in0=ps.ap[:, 0:SB], in1=self.hT[i].ap[:, sl], op=ALU.add),
                              reads=[ps, self.hT[i]], writes=[self.hT[i]])

    def mixer_sconv(self, blk):
        kb = self.kb
        aT, hid = self.aT, self.hid
        self.rmsnorm("gm0")
        for i in range(DC):
            bslot, bw = self.wtile("sc_in", i)
            cslot, cw = self.wtile("sc_in", DC + i)
            xslot, xw = self.wtile("sc_in", 2 * DC + i)
            gs = self.gs[i % 2]
            acc = self.acc[i % 2]
            carry = self.scarry
            kb.act.op(lambda e, gs=gs, i=i: e.copy(out=gs.ap[:, 0:2], in_=carry.ap[:, i, :]), reads=[carry], writes=[gs])
            psb = []
            for s in range(NSUB):
                sl = slice(s * SB, (s + 1) * SB)
                psx = kb.psum()
                kb.mm(psx.ap[:, 0:SB], [(xw[:, c, :], aT[c].ap[:, sl]) for c in range(DC)], reads=[xslot] + aT, pst=psx)
                xs = self.tmp32
                kb.act.op(lambda e, psx=psx, sl=sl: e.activation(out=xs.ap[:, sl], in_=psx.ap[:, 0:SB], func=AF.Copy), reads=[psx], writes=[xs])
                psc = kb.psum()
                kb.mm(psc.ap[:, 0:SB], [(cw[:, c, :], aT[c].ap[:, sl]) for c in range(DC)], reads=[cslot] + aT, pst=psc)
                kb.dve.op(lambda e, gs=gs, psc=psc, s=s, sl=sl: e.tensor_tensor(out=gs.ap[:, 2 + s * SB:2 + (s + 1) * SB], in0=psc.ap[:, 0:SB], in1=xs.ap[:, sl], op=ALU.mult),
                          reads=[psc, xs], writes=[gs])
            for s in range(NSUB):
                sl = slice(s * SB, (s + 1) * SB)
                pb = kb.psum()
                kb.mm(pb.ap[:, 0:SB], [(bw[:, c, :], aT[c].ap[:, sl]) for c in range(DC)], reads=[bslot] + aT, pst=pb)
                psb.append(pb)
            kb.act.op(lambda e, gs=gs, i=i: e.copy(out=carry.ap[:, i, :], in_=gs.ap[:, TB:TB + 2]), reads=[gs], writes=[carry])
            self.conv3(kb.dve, gs, acc, "scw", i, DC)
            for s in range(NSUB):
                sl = slice(s * SB, (s + 1) * SB)
                kb.dve.op(lambda e, acc=acc, i=i, s=s, sl=sl: e.tensor_tensor(out=hid[i].ap[:, sl], in0=psb[s].ap[:, 0:SB], in1=acc.ap[:, sl], op=ALU.mult),
                          reads=[psb[s], acc], writes=[hid[i]])
        self.proj_resid("sc_out", hid, DC)

    def proj_tokmajor(self, wname, t0, evac):
        kb = self.kb
        aT = self.aT
        for n in range(NVG):
            slot, wv = self.wtile(wname, t0 + n)
            for tt in range(TB // 128):
                ps = kb.psum()
                kb.mm(ps.ap[:, 0:NCW], [(aT[c].ap[:, tt * 128:(tt + 1) * 128], wv[:, c, :]) for c in range(DC)], reads=[slot] + aT, pst=ps)
                evac(tt, n, ps)

    def mixer_sgu(self, blk):
        kb = self.kb
        aT, hid = self.aT, self.hid
        self.rmsnorm("gm1")
        NT = TB // 128
        vb = self.vtokbf

        def evac(tt, n, ps):
            kb.act.op(lambda e: e.activation(out=vb[tt].ap[:, n * NCW:(n + 1) * NCW], in_=ps.ap[:, 0:NCW], func=AF.Gelu_apprx_tanh),
                      reads=[ps], writes=[vb[tt]])
        self.proj_tokmajor("sg_inv", 0, evac)
        for tt in range(NT):
            st = self.bnst
            for q in range(4):
                kb.dve.op(lambda e, q=q: e.bn_stats(out=st.ap[:, q, :], in_=vb[tt].ap[:, q * 512:(q + 1) * 512]), reads=[vb[tt]], writes=[st])
            mv = self.bnmv
            kb.dve.op(lambda e: e.bn_aggr(out=mv.ap[:, :], in_=st.ap[:, :, :].rearrange("p q s -> p (q s)")), reads=[st], writes=[mv])
            kb.dve.op(lambda e: e.tensor_scalar(out=mv.ap[:, 1:2], in0=mv.ap[:, 1:2], scalar1=EPS, scalar2=None, op0=ALU.add), reads=[mv], writes=[mv])
            kb.dve.op(lambda e: e.reciprocal(out=mv.ap[:, 1:2], in_=mv.ap[:, 1:2]), reads=[mv], writes=[mv])
            kb.act.op(lambda e: e.activation(out=mv.ap[:, 1:2], in_=mv.ap[:, 1:2], func=AF.Sqrt), reads=[mv], writes=[mv])
            kb.dve.op(lambda e: e.tensor_scalar(out=vb[tt].ap[:, :], in0=vb[tt].ap[:, :], scalar1=mv.ap[:, 0:1], scalar2=mv.ap[:, 1:2],
                                                op0=ALU.subtract, op1=ALU.mult), reads=[vb[tt], mv], writes=[vb[tt]])
        for g in range(DC):
            uslot, uw = self.wtile("sg_inu", g)
            u = self.acc[g % 2]
            for s in range(NSUB):
                sl = slice(s * SB, (s + 1) * SB)
                psu = kb.psum()
                kb.mm(psu.ap[:, 0:SB], [(uw[:, c, :], aT[c].ap[:, sl]) for c in range(DC)], reads=[uslot] + aT, pst=psu)
                kb.act.op(lambda e, u=u, psu=psu, sl=sl: e.activation(out=u.ap[:, sl], in_=psu.ap[:, 0:SB], func=AF.Gelu_apprx_tanh), reads=[psu], writes=[u])
            for s in range(NSUB):
                sl = slice(s * SB, (s + 1) * SB)
                psm = kb.psum()
                nt_s = SB // 128

                def fn(eng, psm=psm, s=s, g=g):
                    ins = None
                    for q in range(nt_s):
                        tt = s * nt_s + q
                        ins = eng.matmul(psm.ap[:, q * 128:(q + 1) * 128], lhsT=vb[tt].ap[:, g * 128:(g + 1) * 128], rhs=self.wsT.ap[:, g, :], start=True, stop=True)
                    return ins
                kb.pe.op(fn, reads=[vb[s * nt_s + q] for q in range(nt_s)] + [self.wsT], writes=[psm])
                t32 = self.tmp32
                kb.dve.op(lambda e, psm=psm, g=g, sl=sl: e.scalar_tensor_tensor(out=t32.ap[:, sl].rearrange("p (a t) -> p a t", t=128), in0=psm.ap[:, 0:SB].rearrange("p (a t) -> p a t", t=128),
                                                                               scalar=self.col("lngc", g), in1=self.bsb.ap[:, g:g + 1, :].broadcast_to([128, nt_s, 128]), op0=ALU.mult, op1=ALU.add),
                          reads=[psm, self.bsb, self.cols], writes=[t32])
                kb.dve.op(lambda e, g=g, u=u, sl=sl: e.tensor_tensor(out=hid[g].ap[:, sl], in0=t32.ap[:, sl], in1=u.ap[:, sl], op=ALU.mult),
                          reads=[t32, u], writes=[hid[g]])
        self.proj_resid("sg_out", hid, DC)

    def qkv(self, wq, wv, gname, qscale, rope, outs, blk):
        kb = self.kb
        aT = self.aT
        qT_d, kT_d, v_d = outs
        self.rmsnorm(gname)
        tok0 = blk * TB
        for which in range(2):
            dst = qT_d if which == 0 else kT_d
            for i in range(DC):
                slot, wv_ = self.wtile(wq, which * DC + i)
                ob = self.qk_out[i % 2]
                for s in range(NSUB):
                    sl = slice(s * SB, (s + 1) * SB)
                    ps = kb.psum()
                    kb.mm(ps.ap[:, 0:SB], [(wv_[:, c, :], aT[c].ap[:, sl]) for c in range(DC)], reads=[slot] + aT, pst=ps)
                    sc = qscale if which == 0 else 1.0
                    if not rope:
                        kb.act.op(lambda e, ob=ob, ps=ps, sl=sl, sc=sc: e.activation(out=ob.ap[:, sl], in_=ps.ap[:, 0:SB], func=AF.Copy, scale=sc), reads=[ps], writes=[ob])
                    else:
                        x32 = self.tmp32
                        xbf = self.tmpbf
                        kb.act.op(lambda e, ps=ps, sl=sl, sc=sc: e.activation(out=x32.ap[:, sl], in_=ps.ap[:, 0:SB], func=AF.Copy, scale=sc), reads=[ps], writes=[x32])
                        kb.act.op(lambda e, ps=ps, sl=sl, sc=sc: e.activation(out=xbf.ap[:, sl], in_=ps.ap[:, 0:SB], func=AF.Copy, scale=sc), reads=[ps], writes=[xbf])
                        ps2 = kb.psum()
                        kb.mm(ps2.ap[:, 0:SB], [(self.rot_bf, xbf.ap[:, sl])], reads=[xbf, self.cst], pst=ps2)
                        gsl = slice(tok0 + s * SB, tok0 + (s + 1) * SB)
                        kb.dve.op(lambda e, sl=sl, gsl=gsl: e.tensor_tensor(out=x32.ap[:, sl], in0=x32.ap[:, sl], in1=self.cosT.ap[:, gsl], op=ALU.mult),
                                  reads=[x32, self.cosT], writes=[x32])
                        t2 = self.tmp32b
                        kb.dve.op(lambda e, sl=sl, gsl=gsl, ps2=ps2: e.tensor_tensor(out=t2.ap[:, sl], in0=ps2.ap[:, 0:SB], in1=self.sinT.ap[:, gsl], op=ALU.mult),
                                  reads=[ps2, self.sinT], writes=[t2])
                        kb.dve.op(lambda e, ob=ob, sl=sl: e.tensor_tensor(out=ob.ap[:, sl], in0=x32.ap[:, sl], in1=t2.ap[:, sl], op=ALU.add),
                                   reads=[x32, t2], writes=[ob])
                if which == 0:
                    kb.act.dma(self.ch_qk[i % 2], dst.ap[i * 128:(i + 1) * 128, tok0:tok0 + TB], ob.ap[:, :], reads=[ob], writes=[dst])
                else:
                    lo = max(HALO - tok0, 0)
                    if lo < TB:
                        kb.sp.dma(self.ch_qk[i % 2], dst.ap[i * 128:(i + 1) * 128, tok0 + lo - HALO:tok0 + TB - HALO], ob.ap[:, lo:TB], reads=[ob], writes=[dst])
        vo = self.v_out

        def evac(tt, n, ps):
            kb.act.op(lambda e: e.activation(out=vo[tt].ap[:, n * NCW:(n + 1) * NCW], in_=ps.ap[:, 0:NCW], func=AF.Copy), reads=[ps], writes=[vo[tt]])
            if n == NVG - 1 and tok0 + tt * 128 >= HALO:
                kb.sp.dma(self.ch_v[tt], v_d.ap[tok0 + tt * 128 - HALO:tok0 + (tt + 1) * 128 - HALO, :], vo[tt].ap[:, :], reads=[vo[tt]], writes=[v_d])
        self.proj_tokmajor(wv, 0, evac)


    @staticmethod
    def is_diag(blk, j):
        tpb = TB // 128
        return any(0 <= j - (16 * r - HALO // 128 + tpb * blk) <= tpb - 1 for r in range(4))

    def attn_setup(self, chm):
        kb = self.kb
        self.kt_slots = [(T(kb.sb(f"kt{i}", [128, 2048], BF16), f"kt{i}"), kb.chan(f"kt{i}")) for i in range(3)]
        self.v_slots = [(T(kb.sb(f"vs{i}", [128, 16, 128], BF16), f"vs{i}"), kb.chan(f"vs{i}")) for i in range(3)]
        self.kt_i = 0
        self.v_i = 0
        self.qpos = T(kb.sb("qpos_sb", [128, NTOK], F32), "qpos")
        self.visb = T(kb.sb("visb_sb", [128, NBLK * NKT], F32), "visb")
        self.sdma(chm, self.qpos.ap[:, :], self.qpos_d.ap.broadcast_to([128, NTOK]), reads=[self.qpos_d], writes=[self.qpos])
        self.sdma(chm, self.visb.ap[:, :], self.visb_d.ap, reads=[self.visb_d], writes=[self.visb])
        self.pt = [T(kb.sb(f"pt{i}", [128, SB], BF16), f"pt{i}") for i in range(4)]
        self.ch_q = [kb.chan(f"q{i}") for i in range(DC)]
        self.acc_ps = kb.ps_tiles[0:4]

    def kv_compact(self, i):
        kb = self.kb
        f0 = self.bsel.ap[:, 0:1]
        f1 = self.bsel.ap[:, 1:2]
        n = 0
        for src8, dst, rpr in ((self.KT8[i], self.KTg[i], D), (self.V8[i], self.Vg[i], OWN)):
            for ch in range(4):
                for rb in range(rpr // 128):
                    a, cha = self.kt_slots[n % 3]
                    b, chb = self.v_slots[n % 3]
                    bflat = b.ap[:, :, :].rearrange("p a b -> p (a b)")
                    r0 = ch * rpr + rb * 128
                    r1 = (4 + ch) * rpr + rb * 128
                    kb.sp.dma(cha, a.ap[:, :], src8.ap[r0:r0 + 128, :], reads=[src8], writes=[a])
                    kb.sp.dma(chb, bflat, src8.ap[r1:r1 + 128, :], reads=[src8], writes=[b])
                    kb.dve.op(lambda e: e.tensor_scalar(out=a.ap[:, :], in0=a.ap[:, :], scalar1=f0, scalar2=None, op0=ALU.mult), reads=[a, self.bsel], writes=[a])
                    kb.dve.op(lambda e: e.scalar_tensor_tensor(out=a.ap[:, :], in0=bflat, scalar=f1, in1=a.ap[:, :], op0=ALU.mult, op1=ALU.add),
                              reads=[a, b, self.bsel], writes=[a])
                    kb.act.dma(cha, dst.ap[r0:r0 + 128, :], a.ap[:, :], reads=[a], writes=[dst])
                    n += 1

    def load_q(self, blk):
        kb = self.kb
        tok0 = blk * TB
        for c in range(DC):
            kb.sp.dma(self.ch_q[c], self.aT[c].ap[:, :], self.qT_d.ap[c * 128:(c + 1) * 128, tok0:tok0 + TB], reads=[self.qT_d], writes=[self.aT[c]])

    def load_kv_chunk(self, h, ch):
        kb = self.kb
        kt, kch = self.kt_slots[self.kt_i % 3]
        self.kt_i += 1
        vs, vch = self.v_slots[self.v_i % 3]
        self.v_i += 1
        kb.sp.dma(kch, kt.ap[:, :], self.KT_d.ap[ch * D + h * 128:ch * D + (h + 1) * 128, :], reads=[self.KT_d], writes=[kt])
        vsrc = self.V_d.ap[ch * 2048:(ch + 1) * 2048, h * 128:(h + 1) * 128].rearrange("(j k) e -> k j e", k=128)
        kb.sp.dma(vch, vs.ap[:, :, :], vsrc, reads=[self.V_d], writes=[vs])
        return kt, vs

    def attn_diff(self, blk):
        kb = self.kb
        tok0 = blk * TB
        self.load_q(blk)
        qpos_blk = self.qpos.ap[:, tok0:tok0 + TB]
        kb.ps_pool = [4, 5, 6, 7]
        accO = [self.acc_ps[0], self.acc_ps[1]]
        accL = [self.acc_ps[2], self.acc_ps[3]]
        for h in range(16):
            qh = self.aT[h]
            for j in range(NKT):
                jj = j % 16
                if jj == 0:
                    kt, vs = self.load_kv_chunk(h, j // 16)
                diag = True
                bias = self.visb.ap[:, blk * NKT + j:blk * NKT + j + 1]
                for c in range(2):
                    ps = kb.psum()
                    kb.mm(ps.ap[:, 0:SB], [(kt.ap[c * 64:(c + 1) * 64, jj * 128:(jj + 1) * 128], qh.ap[c * 64:(c + 1) * 64, :])], reads=[kt, qh], pst=ps)
                    p = self.pt[(2 * j + c) % 4]
                    kb.act.op(lambda e, p=p, ps=ps: e.activation(out=p.ap[:, :], in_=ps.ap[:, 0:SB], func=AF.Exp, bias=bias, scale=1.0),
                              reads=[ps, self.visb], writes=[p])
                    if diag:
                        kb.dve.op(lambda e, p=p: e.scalar_tensor_tensor(out=p.ap[:, :], in0=qpos_blk, scalar=self.col("kpos", j), in1=p.ap[:, :], op0=ALU.is_ge, op1=ALU.mult),
                                  reads=[p, self.qpos, self.cols], writes=[p])
                    kb.pe.op(lambda e, p=p, c=c: e.matmul(accO[c].ap[:, 0:SB], lhsT=vs.ap[:, jj, :], rhs=p.ap[:, :], start=(j == 0), stop=(j == NKT - 1)),
                             reads=[vs, p], writes=[accO[c]])
                    kb.pe.op(lambda e, p=p, c=c: e.matmul(accL[c].ap[:, 0:SB], lhsT=self.ones_bf, rhs=p.ap[:, :], start=(j == 0), stop=(j == NKT - 1)),
                             reads=[self.cst, p], writes=[accL[c]])
                    if self.phases == "dbgC" and blk == 1 and h == 0 and j in (0, 1) and c == 0:
                        st = self.dbgst[4 + j]
                        kb.dve.op(lambda e, p=p, st=st: e.tensor_copy(out=st.ap[:, :], in_=p.ap[:, :]), reads=[p], writes=[st])
            if self.phases == "dbgC" and blk == 1 and h == 0:
                for i4, src in enumerate([accO[0], accO[1], accL[0], accL[1]]):
                    st = self.dbgst[i4]
                    kb.dve.op(lambda e, st=st, src=src: e.tensor_copy(out=st.ap[:, :], in_=src.ap[:, 0:SB]), reads=[src], writes=[st])
                chd = kb.chan("dbgp")
                for i6 in range(6):
                    kb.sp.dma(chd, self.dbgp.ap[i6], self.dbgst[i6].ap[:, :], reads=[self.dbgst[i6]], writes=[self.dbgp])
                    kb.sync_chan(chd)
                kb.ps_pool = list(range(8))
                return "stop"
            rl = self.rl
            for c in range(2):
                kb.dve.op(lambda e, c=c: e.tensor_scalar(out=rl[c].ap[:, :], in0=accL[c].ap[:, 0:SB], scalar1=1e-30, scalar2=None, op0=ALU.max), reads=[accL[c]], writes=[rl[c]])
                kb.dve.op(lambda e, c=c: e.reciprocal(out=rl[c].ap[:, :], in_=rl[c].ap[:, :]), reads=[rl[c]], writes=[rl[c]])
            o, t = self.tmp32, self.tmp32b
            kb.dve.op(lambda e: e.tensor_tensor(out=o.ap[:, :], in0=accO[0].ap[:, 0:SB], in1=rl[0].ap[:, :], op=ALU.mult), reads=[accO[0], rl[0]], writes=[o])
            kb.dve.op(lambda e: e.tensor_tensor(out=t.ap[:, :], in0=accO[1].ap[:, 0:SB], in1=rl[1].ap[:, :], op=ALU.mult), reads=[accO[1], rl[1]], writes=[t])
            kb.dve.op(lambda e: e.scalar_tensor_tensor(out=o.ap[:, :], in0=t.ap[:, :], scalar=self.lamc.ap[:, 0:1], in1=o.ap[:, :], op0=ALU.mult, op1=ALU.add),
                      reads=[t, o, self.lamc], writes=[o])
            sq = self.tmpbf
            kb.dve.op(lambda e: e.tensor_tensor(out=sq.ap[:, :], in0=o.ap[:, :], in1=o.ap[:, :], op=ALU.mult), reads=[o], writes=[sq])
            ps = kb.psum()
            kb.mm(ps.ap[:, 0:SB], [(self.ones_bf, sq.ap[:, :])], reads=[self.cst, sq], pst=ps)
            r = self.rstd
            kb.dve.op(lambda e: e.tensor_scalar(out=r.ap[:, :], in0=ps.ap[:, 0:SB], scalar1=1.0 / 128.0, scalar2=EPS, op0=ALU.mult, op1=ALU.add), reads=[ps], writes=[r])
            kb.dve.op(lambda e: e.reciprocal(out=r.ap[:, :], in_=r.ap[:, :]), reads=[r], writes=[r])
            kb.act.op(lambda e: e.activation(out=r.ap[:, :], in_=r.ap[:, :], func=AF.Sqrt), reads=[r], writes=[r])
            kb.dve.op(lambda e, h=h: e.scalar_tensor_tensor(out=self.hid[h].ap[:, :], in0=o.ap[:, :], scalar=self.lamc.ap[:, 1:2], in1=r.ap[:, :], op0=ALU.mult, op1=ALU.mult),
                      reads=[o, r, self.lamc], writes=[self.hid[h]])
        kb.ps_pool = list(range(8))

    def lam_setup(self, chm, lambda_init):
        kb = self.kb
        lv = T(kb.sb("lamv", [128, 4, 64], F32), "lamv")
        for i, d in enumerate(self.lam_d):
            self.sdma(chm, lv.ap[:, i, :], d.ap.broadcast_to([128, 64]), reads=[d], writes=[lv])
        pr = T(kb.sb("lampr", [128, 2, 64], F32), "lampr")
        sm = T(kb.sb("lamsm", [128, 2], F32), "lamsm")
        self.lamc = T(kb.sb("lamc", [128, 2], F32), "lamc")
        for i in range(2):
            kb.dve.op(lambda e, i=i: e.tensor_tensor(out=pr.ap[:, i, :], in0=lv.ap[:, 2 * i, :], in1=lv.ap[:, 2 * i + 1, :], op=ALU.mult), reads=[lv], writes=[pr])
            kb.dve.op(lambda e, i=i: e.reduce_sum(out=sm.ap[:, i:i + 1], in_=pr.ap[:, i, :], axis=AX.X), reads=[pr], writes=[sm])
        kb.act.op(lambda e: e.activation(out=sm.ap[:, :], in_=sm.ap[:, :], func=AF.Exp), reads=[sm], writes=[sm])
        kb.dve.op(lambda e: e.tensor_tensor(out=self.lamc.ap[:, 0:1], in0=sm.ap[:, 1:2], in1=sm.ap[:, 0:1], op=ALU.subtract), reads=[sm], writes=[self.lamc])
        kb.dve.op(lambda e: e.tensor_scalar(out=self.lamc.ap[:, 0:1], in0=self.lamc.ap[:, 0:1], scalar1=-lambda_init, scalar2=None, op0=ALU.add), reads=[self.lamc], writes=[self.lamc])
        kb.dve.op(lambda e: e.tensor_scalar(out=self.lamc.ap[:, 1:2], in0=self.col("subg"), scalar1=1.0 - lambda_init, scalar2=None, op0=ALU.mult), reads=[self.cols, self.lamc], writes=[self.lamc])

    def attn_sb(self, blk):
        kb = self.kb
        tok0 = blk * TB
        self.load_q(blk)
        qpos_blk = self.qpos.ap[:, tok0:tok0 + TB]
        kb.ps_pool = [1, 2, 3, 4, 5, 6, 7]
        accO = self.acc_ps[0]
        R = self.Racc
        tincl = self.cst32.ap[:, 3, :]
        ones32 = self.cst32.ap[:, 0, :]
        step = 0
        for h in range(16):
            qh = self.aT[h]
            kb.dve.op(lambda e: e.memset(R.ap[:, :], 0.0), writes=[R])
            for j in reversed(range(NKT)):
                jj = j % 16
                if jj == 15:
                    kt, vs = self.load_kv_chunk(h, j // 16)
                diag = True
                bias = self.visb.ap[:, blk * NKT + j:blk * NKT + j + 1]
                kcol = self.col("kpos", j)
                psz = kb.psum()
                kb.mm(psz.ap[:, 0:SB], [(kt.ap[:, jj * 128:(jj + 1) * 128], qh.ap[:, :])], reads=[kt, qh], pst=psz)
                e32 = self.e32[step % 2]
                L = self.L32[step % 2]
                t = self.t32[step % 2]
                A = self.pt[step % 4]
                kb.act.op(lambda e: e.activation(out=e32.ap[:, :], in_=psz.ap[:, 0:SB], func=AF.Exp, bias=bias, scale=1.0), reads=[psz, self.visb], writes=[e32])
                if diag:
                    kb.dve.op(lambda e: e.scalar_tensor_tensor(out=e32.ap[:, :], in0=qpos_blk, scalar=kcol, in1=e32.ap[:, :], op0=ALU.is_gt, op1=ALU.mult),
                              reads=[e32, self.qpos, self.cols], writes=[e32])
                kb.act.op(lambda e: e.activation(out=e32.ap[:, :], in_=e32.ap[:, :], func=AF.Ln, bias=1.0, scale=1.0), reads=[e32], writes=[e32])
                kb.dve.op(lambda e: e.tensor_copy(out=L.ap[:, :], in_=e32.ap[:, :]), reads=[e32], writes=[L])
                psc = kb.psum()
                kb.mm(psc.ap[:, 0:SB], [(tincl, L.ap[:, :])], reads=[self.cst32, L], pst=psc)
                pss = kb.psum()
                kb.mm(pss.ap[:, 0:SB], [(ones32, L.ap[:, :])], reads=[self.cst32, L], pst=pss)
                kb.dve.op(lambda e: e.tensor_tensor(out=t.ap[:, :], in0=psc.ap[:, 0:SB], in1=R.ap[:, :], op=ALU.add), reads=[psc, R], writes=[t])
                kb.dve.op(lambda e: e.tensor_tensor(out=t.ap[:, :], in0=psz.ap[:, 0:SB], in1=t.ap[:, :], op=ALU.subtract), reads=[psz, t], writes=[t])
                kb.act.op(lambda e: e.activation(out=A.ap[:, :], in_=t.ap[:, :], func=AF.Exp, bias=bias, scale=1.0), reads=[t, self.visb], writes=[A])
                if diag:
                    kb.dve.op(lambda e: e.scalar_tensor_tensor(out=A.ap[:, :], in0=qpos_blk, scalar=kcol, in1=A.ap[:, :], op0=ALU.is_gt, op1=ALU.mult),
                              reads=[A, self.qpos, self.cols], writes=[A])
                kb.dve.op(lambda e: e.tensor_tensor(out=R.ap[:, :], in0=pss.ap[:, 0:SB], in1=R.ap[:, :], op=ALU.add), reads=[pss, R], writes=[R])
                kb.pe.op(lambda e: e.matmul(accO.ap[:, 0:SB], lhsT=vs.ap[:, jj, :], rhs=A.ap[:, :], start=(j == NKT - 1), stop=(j == 0)),
                         reads=[vs, A], writes=[accO])
                step += 1
            kb.dve.op(lambda e, h=h: e.tensor_copy(out=self.hid[h].ap[:, :], in_=accO.ap[:, 0:SB]), reads=[accO], writes=[self.hid[h]])
        kb.ps_pool = list(range(8))

    def load_h(self, src, blk):
        kb = self.kb
        tok0 = blk * TB
        for c in range(DC):
            kb.sp.dma(self.ch_h[c], self.hT[c].ap[:, :], src.ap[c * 128:(c + 1) * 128, tok0:tok0 + TB], reads=[src], writes=[self.hT[c]])

    def store_h(self, dst, blk, owned_only=False):
        kb = self.kb
        tok0 = blk * TB
        for c in range(DC):
            if not owned_only:
                kb.act.dma(self.ch_h[c], dst.ap[c * 128:(c + 1) * 128, tok0:tok0 + TB], self.hT[c].ap[:, :], reads=[self.hT[c]], writes=[dst])
            else:
                lo = max(HALO - tok0, 0)
                if lo < TB:
                    kb.act.dma(self.ch_h[c], dst.ap[c * 128:(c + 1) * 128, tok0 + lo - HALO:tok0 + TB - HALO], self.hT[c].ap[:, lo:TB], reads=[self.hT[c]], writes=[dst])

    def dump_dbg(self, i):
        kb = self.kb
        kb.barrier()
        chd = kb.chan("dbg")
        kb.sp.dma(chd, self.dbgh.ap, self.hS.ap, reads=[self.hS], writes=[self.dbgh])
        kb.sync_chan(chd)
        kb.sp.dma(chd, self.dbgk.ap, self.KTg[i].ap[:, OWN - 256:OWN], reads=[self.KTg[i]], writes=[self.dbgk])
        kb.sync_chan(chd)
        kb.sp.dma(chd, self.dbgv.ap, self.Vg[i].ap[:, D - 256:D], reads=[self.Vg[i]], writes=[self.dbgv])
        kb.sync_chan(chd)
        if self.phases == "dbgA":
            for dst, src in ((self.dbgq, self.qS[i]), (self.dbgK, self.KTg[i]), (self.dbgV, self.Vg[i])):
                kb.sp.dma(chd, dst.ap, src.ap, reads=[src], writes=[dst])
                kb.sync_chan(chd)
        kb.barrier()
        kb.close()

    def sdma(self, chm, out_ap, in_ap, reads=(), writes=()):
        self.kb.sp.dma(chm, out_ap, in_ap, reads=reads, writes=writes)
        self.kb.sync_chan(chm)

    def build(self):
        kb = self.kb
        nc = kb.nc
        self.wts = {}
        NT = TB // 128
        G4 = [[0, 1, 2, 3], [4, 5, 6, 7]]
        self.cols_d = kb.din("cols", [128, NCOLS], F32)
        self.cst_d = kb.din("consts", [128, 5, 128], F32)
        self.xT = kb.din("xT", [D, NTOK], F32)
        self.declare_weight("sc_in", D, 3 * D, "T")
        self.declare_weight("sc_out", D, D, "T")
        self.declare_weight("wg0", D, DFF, "T")
        self.declare_weight("wu0", D, DFF, "T")
        self.declare_weight("wd0h0", DFF // 2, D, "T")
        self.declare_weight("wd0h1", DFF // 2, D, "T")
        self.declare_weight("sg_inv", D, D, "N")
        self.declare_weight("sg_inu", D, D, "T")
        self.declare_weight("sg_out", D, D, "T")
        self.declare_weight("wg1", D, DFF, "T")
        self.declare_weight("wu1", D, DFF, "T")
        self.declare_weight("wd1h0", DFF // 2, D, "T")
        self.declare_weight("wd1h1", DFF // 2, D, "T")
        self.declare_weight("da_qk", D, 2 * D, "T")
        self.declare_weight("da_v", D, D, "N")
        self.declare_weight("da_out", D, D, "T")
        self.declare_weight("wg2", D, DFF, "T")
        self.declare_weight("wu2", D, DFF, "T")
        self.declare_weight("wd2h0", DFF // 2, D, "T")
        self.declare_weight("wd2h1", DFF // 2, D, "T")
        self.declare_weight("sb_qk", D, 2 * D, "T")
        self.declare_weight("sb_v", D, D, "N")
        self.declare_weight("sb_out", D, D, "T")
        self.declare_weight("wg3", D, DFF, "T")
        self.declare_weight("wu3", D, DFF, "T")
        self.declare_weight("wd3h0", DFF // 2, D, "T")
        self.declare_weight("wd3h1", DFF // 2, D, "T")
        self.wsT_d = kb.din("sg_wsT", [128, DC, 128], F32)
        self.bs_d = kb.din("sg_bs", [1, DC * 128], F32)
        self.pos_d = kb.din("pos", [1, NTOK], I32)
        self.qpos_d = kb.din("qpos", [1, NTOK], F32)
        self.visb_d = kb.din("visb", [128, NBLK * NKT], F32)
        self.lam_d = [kb.din(f"lam{i}", [1, 64], F32) for i in range(4)]
        self.outT = kb.dout("outT", [D, OWN], F32)
        self.hS = kb.dscr("hS", [D, NTOK], F32)
        self.qS = [kb.dscr(f"qS{i}", [D, NTOK], BF16) for i in range(2)]
        self.kown = [kb.dscr(f"kown{i}", [D, OWN], BF16) for i in range(2)]
        self.vown = [kb.dscr(f"vown{i}", [OWN, D], BF16) for i in range(2)]
        self.KTg = [kb.dscr(f"KTg{i}", [4 * D, OWN], BF16) for i in range(2)]
        self.Vg = [kb.dscr(f"Vg{i}", [4 * OWN, D], BF16) for i in range(2)]
        self.KT8 = [kb.dscr(f"KT8_{i}", [8 * D, OWN], BF16) for i in range(2)]
        self.V8 = [kb.dscr(f"V8_{i}", [8 * OWN, D], BF16) for i in range(2)]
        self.bsel_d = kb.din("bsel", [128, 2], F32)
        if self.phases.startswith("dbg"):
            self.dbg = kb.dout("dbg", [len(self.wts), 128, 2048], BF16)
            self.dbgh = kb.dout("dbgh", [D, NTOK], F32)
            self.dbgk = kb.dout("dbgk", [4 * D, 256], BF16)
            self.dbgv = kb.dout("dbgv", [4 * OWN, 256], BF16)
            self.dbgp = kb.dout("dbgp", [8, 128, SB], F32)
            if self.phases == "dbgA":
                self.dbgq = kb.dout("dbgq", [D, NTOK], BF16)
                self.dbgK = kb.dout("dbgK", [4 * D, OWN], BF16)
                self.dbgV = kb.dout("dbgV", [4 * OWN, D], BF16)
        kb.init_psum()
        self.convert_weights(list(self.wts.keys()))
        if self.phases == "dbgE1":
            kb.allgather([[0, 1, 2, 3], [4, 5, 6, 7]], self.kown[0], self.KTg[0])
            kb.allgather([[0, 1, 2, 3], [4, 5, 6, 7]], self.vown[0], self.Vg[0])
            self.dump_dbg(0)
            return
        if self.phases == "dbgpre":
            chd = kb.chan("dbg")
            for i, (name, (w, loc, full, K, N, mode)) in enumerate(self.wts.items()):
                nrows = full.ap.shape[0]
                kb.sp.dma(chd, self.dbg.ap[i], full.ap[nrows - 128:nrows, 0:2048], reads=[full], writes=[self.dbg])
                kb.sync_chan(chd)
            kb.barrier()
            kb.close()
            return
        self.cols = T(kb.sb("cols_sb", [128, NCOLS], F32), "cols")
        cst32 = T(kb.sb("cst32", [128, 5, 128], F32), "cst32")
        self.cst32 = cst32
        self.cst = T(kb.sb("cst", [128, 5, 128], BF16), "cst")
        chm = kb.chan("misc")
        self.sdma(chm, self.cols.ap[:, :], self.cols_d.ap, reads=[self.cols_d], writes=[self.cols])
        self.sdma(chm, cst32.ap[:, :, :], self.cst_d.ap, reads=[self.cst_d], writes=[cst32])
        kb.dve.op(lambda e: e.tensor_copy(out=self.cst.ap[:, :, :], in_=cst32.ap[:, :, :]), reads=[cst32], writes=[self.cst])
        self.ones_bf = self.cst.ap[:, 0, :]
        self.rot_bf = self.cst.ap[:, 2, :]
        self.hT = [T(kb.sb(f"hT{c}", [128, TB], F32), f"hT{c}") for c in range(DC)]
        self.aT = [T(kb.sb(f"aT{c}", [128, TB], BF16), f"aT{c}") for c in range(DC)]
        self.hid = [T(kb.sb(f"hid{c}", [128, TB], BF16), f"hid{c}") for c in range(FC // 2)]
        self.rstd = T(kb.sb("rstd", [128, TB], F32), "rstd")
        self.tmp32 = T(kb.sb("tmp32", [128, TB], F32), "tmp32")
        self.tmp32b = T(kb.sb("tmp32b", [128, TB], F32), "tmp32b")
        self.tmpbf = T(kb.sb("tmpbf", [128, TB], BF16), "tmpbf")
        self.gs = [T(kb.sb(f"gs{i}", [128, TB + 2], F32), f"gs{i}") for i in range(2)]
        self.acc = [T(kb.sb(f"acc{i}", [128, TB], F32), f"acc{i}") for i in range(2)]
        self.fcarrys = {}
        for l in range(4):
            t = T(kb.sb(f"fcarry{l}", [128, FC, 2], F32), f"fcarry{l}")
            self.fcarrys[l] = t
            kb.dve.op(lambda e, t=t: e.memset(t.ap[:, :, :], 0.0), writes=[t])
        self.scarry = T(kb.sb("scarry", [128, DC, 2], F32), "scarry")
        kb.dve.op(lambda e: e.memset(self.scarry.ap[:, :, :], 0.0), writes=[self.scarry])
        self.wsmall = [(T(kb.sb(f"ws{i}", [128, 2048], BF16), f"ws{i}"), kb.chan(f"ws{i}")) for i in range(5)]
        self.wbig = [(T(kb.sb(f"wb{i}", [128, 4096], BF16), f"wb{i}"), kb.chan(f"wb{i}")) for i in range(3)]
        self.wsmall_i = 0
        self.wbig_i = 0
        self.ch_h = [kb.chan(f"h{i}") for i in range(DC)]
        self.qk_out = [T(kb.sb(f"qko{i}", [128, TB], BF16), f"qko{i}") for i in range(2)]
        self.v_out = [T(kb.sb(f"vo{i}", [128, D], BF16), f"vo{i}") for i in range(NT)]
        self.ch_qk = [kb.chan(f"qk{i}") for i in range(2)]
        self.ch_v = [kb.chan(f"v{i}") for i in range(NT)]

        mA = kb.mark()
        self.vtokbf = [T(kb.sb(f"vb{i}", [128, D], BF16), f"vb{i}") for i in range(NT)]
        self.bnst = T(kb.sb("bnst", [128, 4, nc.vector.BN_STATS_DIM], F32), "bnst")
        self.bnmv = T(kb.sb("bnmv", [128, nc.vector.BN_AGGR_DIM], F32), "bnmv")
        self.bsb = T(kb.sb("bsb", [128, DC, 128], F32), "bsb")
        ws32 = T(kb.sb("ws32", [128, DC, 128], F32), "ws32")
        self.wsT = T(kb.sb("wsT", [128, DC, 128], BF16), "wsT")
        self.sdma(chm, self.bsb.ap[:, :, :].rearrange("p g t -> p (g t)"), self.bs_d.ap.broadcast_to([128, DC * 128]), reads=[self.bs_d], writes=[self.bsb])
        self.sdma(chm, ws32.ap[:, :, :], self.wsT_d.ap, reads=[self.wsT_d], writes=[ws32])
        kb.dve.op(lambda e: e.tensor_tensor(out=self.wsT.ap[:, :, :], in0=ws32.ap[:, :, :],
                                            in1=cst32.ap[:, 1:2, :].broadcast_to([128, DC, 128]), op=ALU.mult),
                  reads=[ws32, cst32], writes=[self.wsT])
        wflat = self.wsT.ap[:, :, :].rearrange("p g t -> p (g t)")
        for q in range(4):
            ps = kb.psum()
            kb.mm(ps.ap[:, 0:512], [(self.ones_bf, wflat[:, q * 512:(q + 1) * 512])], reads=[self.cst, self.wsT], pst=ps)
            for gg in range(4):
                g = q * 4 + gg
                kb.dve.op(lambda e, g=g, gg=gg, ps=ps: e.scalar_tensor_tensor(out=self.bsb.ap[:, g, :], in0=ps.ap[:, gg * 128:(gg + 1) * 128], scalar=self.col("lnbc", g),
                                                                              in1=self.bsb.ap[:, g, :], op0=ALU.mult, op1=ALU.add),
                          reads=[ps, self.bsb, self.cols], writes=[self.bsb])
        self.cosT = T(kb.sb("cosT", [128, NTOK], F32), "cosT")
        self.sinT = T(kb.sb("sinT", [128, NTOK], F32), "sinT")
        posi = T(kb.sb("posi", [128, TB], I32), "posi")
        u = self.tmp32
        r = self.tmp32b
        for ch in range(NBLK):
            csl = slice(ch * TB, (ch + 1) * TB)
            self.sdma(chm, posi.ap[:, :], self.pos_d.ap[:, csl].broadcast_to([128, TB]), reads=[self.pos_d], writes=[posi])
            kb.dve.op(lambda e: e.tensor_copy(out=u.ap[:, :], in_=posi.ap[:, :]), reads=[posi], writes=[u])
            kb.dve.op(lambda e: e.tensor_scalar(out=u.ap[:, :], in0=u.ap[:, :], scalar1=self.col("invf"), scalar2=None, op0=ALU.mult),
                      reads=[u, self.cols], writes=[u])
            for tab, shift in ((self.sinT, 0.0), (self.cosT, 0.25)):
                kb.dve.op(lambda e, shift=shift: e.tensor_scalar(out=r.ap[:, :], in0=u.ap[:, :], scalar1=shift, scalar2=None, op0=ALU.add), reads=[u], writes=[r])
                kb.dve.op(lambda e: e.tensor_copy(out=posi.ap[:, :], in_=r.ap[:, :]), reads=[r], writes=[posi])
                kb.dve.op(lambda e, tab=tab: e.tensor_copy(out=tab.ap[:, csl], in_=posi.ap[:, :]), reads=[posi], writes=[tab])
                kb.dve.op(lambda e, tab=tab: e.tensor_tensor(out=r.ap[:, :], in0=r.ap[:, :], in1=tab.ap[:, csl], op=ALU.subtract), reads=[r, tab], writes=[r])
                kb.act.op(lambda e, tab=tab: e.activation(out=tab.ap[:, csl], in_=r.ap[:, :], func=AF.Sin, scale=2.0 * math.pi), reads=[r], writes=[tab])
        for blk in range(NBLK):
            self.load_h(self.xT, blk)
            self.mixer_sconv(blk)
            self.ffn(0, blk)
            self.mixer_sgu(blk)
            self.ffn(1, blk)
            self.store_h(self.hS, blk)
            self.qkv("da_qk", "da_v", "gm2", 64 ** -0.5, True, (self.qS[0], self.kown[0], self.vown[0]), blk)
        G8 = [list(range(NCORE))]
        kb.allgather(G8, self.kown[0], self.KT8[0])
        kb.allgather(G8, self.vown[0], self.V8[0])
        kb.release(mA)

        self.attn_setup(chm)
        self.lam_setup(chm, 0.8 - 0.6 * math.exp(-0.3 * 2))
        self.rl = [T(kb.sb(f"rl{i}", [128, SB], F32), f"rl{i}") for i in range(2)]
        self.e32 = [T(kb.sb(f"e32{i}", [128, SB], F32), f"e32{i}") for i in range(2)]
        self.L32 = [T(kb.sb(f"L32{i}", [128, SB], F32), f"L32{i}") for i in range(2)]
        self.t32 = [T(kb.sb(f"t32{i}", [128, SB], F32), f"t32{i}") for i in range(2)]
        self.Racc = T(kb.sb("Racc", [128, SB], F32), "Racc")
        self.bsel = T(kb.sb("bsel_sb", [128, 2], F32), "bsel")
        self.sdma(chm, self.bsel.ap[:, :], self.bsel_d.ap, reads=[self.bsel_d], writes=[self.bsel])
        self.kv_compact(0)
        if self.phases == "dbgA":
            self.dump_dbg(0)
            return
        self.qT_d, self.KT_d, self.V_d = self.qS[0], self.KTg[0], self.Vg[0]
        if self.phases == "dbgC":
            self.dbgst = [T(kb.sb(f"dbgst{i}", [128, SB], F32), f"dbgst{i}") for i in range(6)]
        for blk in range(NBLK):
            self.load_h(self.hS, blk)
            if self.attn_diff(blk) == "stop":
                kb.barrier()
                kb.close()
                return
            self.proj_resid("da_out", self.hid, DC)
            self.ffn(2, blk)
            self.store_h(self.hS, blk)
            self.qkv("sb_qk", "sb_v", "gm3", 128 ** -0.5, False, (self.qS[1], self.kown[1], self.vown[1]), blk)
        kb.allgather(G8, self.kown[1], self.KT8[1])
        kb.allgather(G8, self.vown[1], self.V8[1])
        self.kv_compact(1)
        if self.phases == "dbgB":
            self.dump_dbg(1)
            return
        self.qT_d, self.KT_d, self.V_d = self.qS[1], self.KTg[1], self.Vg[1]
        for blk in range(NBLK):
            self.load_h(self.hS, blk)
            self.attn_sb(blk)
            self.proj_resid("sb_out", self.hid, DC)
            self.ffn(3, blk)
            self.rmsnorm("gfin", inplace=True)
            self.store_h(self.outT, blk, owned_only=True)
        kb.barrier()
        kb.close()


_PROG = None


def get_prog(ph="ABC"):
    global _PROG
    if _PROG is None:
        _PROG = Prog(ph)
    return _PROG


def core_layout(inputs):
    x = np.asarray(inputs["x"])
    pos = np.asarray(inputs["positions"])
    xs, ps = [], []
    for core in range(NCORE):
        b, r = core // 4, core % 4
        t0 = r * OWN - HALO
        xt = np.zeros((D, NTOK), np.float32)
        pp = np.zeros((1, NTOK), np.int32)
        lo = max(t0, 0)
        xt[:, lo - t0:] = x[b, lo:t0 + NTOK, :].T
        pp[0, lo - t0:] = pos[b, lo:t0 + NTOK]
        xs.append(np.ascontiguousarray(xt))
        ps.append(pp)
    return xs, ps


def qpos_visb(core):
    r = core % 4
    t0 = r * OWN - HALO
    qpos = (np.arange(NTOK, dtype=np.float32) + np.float32(t0))[None, :]
    visb = np.zeros((128, NBLK * NKT), np.float32)
    for b in range(NBLK):
        qmax = t0 + (b + 1) * TB - 1
        for j in range(NKT):
            if 128 * j > qmax:
                visb[:, b * NKT + j] = NEG
    return qpos, visb


def weight_table(inp):
    w = {}
    w["sc_in"] = inp["sc_w_in"][0]
    w["sc_out"] = inp["sc_w_out"][0]
    w["sg_inu"] = inp["sg_w_in"][0][:, :D]
    w["sg_inv"] = inp["sg_w_in"][0][:, D:]
    w["sg_out"] = inp["sg_w_out"][0]
    w["da_qk"] = inp["da_w_qkv"][0][:, :2 * D]
    w["da_v"] = inp["da_w_qkv"][0][:, 2 * D:]
    w["da_out"] = inp["da_w_out"][0]
    w["sb_qk"] = inp["sb_w_qkv"][0][:, :2 * D]
    w["sb_v"] = inp["sb_w_qkv"][0][:, 2 * D:]
    w["sb_out"] = inp["sb_w_out"][0]
    for l in range(4):
        w[f"wg{l}"] = inp["ffn_w_gate"][l]
        w[f"wu{l}"] = inp["ffn_w_up"][l]
        w[f"wd{l}h0"] = inp["ffn_w_down"][l][:DFF // 2]
        w[f"wd{l}h1"] = inp["ffn_w_down"][l][DFF // 2:]
    return w


def shard_cols(a, npad, core):
    K, N = a.shape
    n = npad // NCORE
    lo, hi = core * n, (core + 1) * n
    out = np.zeros((K, n), np.float32)
    if lo < N:
        out[:, :min(hi, N) - lo] = a[:, lo:min(hi, N)]
    return out


def kernel(_ph="ABC", **inputs):
    inp = {k: np.asarray(v) for k, v in inputs.items()}
    prog = get_prog(_ph)
    xs, ps = core_layout(inp)
    wt = weight_table(inp)
    common = {"cols": build_cols(inp), "consts": build_consts(),
              "sg_wsT": np.ascontiguousarray(np.transpose(inp["sg_w_s"][0], (2, 0, 1))),
              "sg_bs": np.ascontiguousarray(inp["sg_b_s"][0].reshape(1, -1))}
    for i, k in enumerate(["da_lambda_q1", "da_lambda_k1", "da_lambda_q2", "da_lambda_k2"]):
        common[f"lam{i}"] = np.ascontiguousarray(inp[k][0][None, :])
    in_maps = []
    for core in range(NCORE):
        m = dict(common)
        m["xT"] = xs[core]
        m["pos"] = ps[core]
        m["qpos"], m["visb"] = qpos_visb(core)
        bs = np.zeros((128, 2), np.float32)
        bs[:, core // 4] = 1.0
        m["bsel"] = bs
        for name, (w, loc, full, K, N, mode) in prog.wts.items():
            m[name] = shard_cols(wt[name], N, core)
        in_maps.append(m)
    res = run_bass_kernel_spmd(prog.kb.nc, in_maps, core_ids=list(range(NCORE)))
    if _ph.startswith("dbg"):
        return res.results, in_maps
    out = np.zeros((2, SEQ, D), np.float32)
    for core in range(NCORE):
        b, r = core // 4, core % 4
        out[b, r * OWN:(r + 1) * OWN, :] = np.asarray(res.results[core]["outT"]).T
    return out
```
